# Optimizing a Trainium2 kernel written in Bass

```python
import jax, jax.numpy as jnp
from jax import lax
import numpy as np

D_MODEL = 1024
BATCH = 8
SEQ = 4096
DEPTH = 2

D_MIX = D_MODEL
D_A = D_MIX // 2
D_B = D_MIX - D_A
CHUNK = 128
H_A = 4
HD_A = D_A // H_A
H_B = 4
HD_B = D_B // H_B
CONV_W = 31
D_FF = 4 * D_MODEL
IN_COLS = 2 * D_A + 2 * D_B
EPS = 1e-6

kernel_name = "hybrid_sgu_conformer_conv_block"


def rms_norm(x, g):
    xf = x.astype(jnp.float32)
    y = xf * lax.rsqrt(jnp.mean(xf * xf, axis=-1, keepdims=True) + EPS)
    return (y * g.astype(jnp.float32)).astype(x.dtype)


def layer_norm(x, g, b):
    xf = x.astype(jnp.float32)
    mu = jnp.mean(xf, axis=-1, keepdims=True)
    var = jnp.mean(jnp.square(xf - mu), axis=-1, keepdims=True)
    y = (xf - mu) * lax.rsqrt(var + EPS)
    return (y * g.astype(jnp.float32) + b.astype(jnp.float32)).astype(x.dtype)


def spatial_gating(u_a, v_a, ln_g, ln_b, w_s, b_s):
    bsz, seq, _ = u_a.shape
    u = jax.nn.gelu(u_a, approximate=False)
    v = layer_norm(jax.nn.gelu(v_a, approximate=False), ln_g, ln_b)
    v = v.reshape(bsz, seq // CHUNK, CHUNK, H_A, HD_A)
    mask = jnp.tril(jnp.ones((CHUNK, CHUNK), dtype=w_s.dtype))
    w = w_s * mask[None]
    mixed = jnp.einsum('hts,bcshd->bcthd', w, v)
    mixed = mixed + jnp.transpose(b_s)[None, None, :, :, None]
    return u * mixed.reshape(bsz, seq, D_A)


def conformer_conv(val_b, gate_b, conv_w, conv_b, ln_g, ln_b):
    bsz, seq, _ = val_b.shape
    g = val_b * jax.nn.sigmoid(gate_b)
    c = lax.conv_general_dilated(
        g, conv_w[:, None, :].astype(g.dtype),
        window_strides=(1,), padding=[(CONV_W - 1, 0)],
        dimension_numbers=('NWC', 'WIO', 'NWC'),
        feature_group_count=D_B)
    c = c + conv_b
    c = layer_norm(c.reshape(bsz, seq, H_B, HD_B),
                   ln_g.reshape(H_B, HD_B), ln_b.reshape(H_B, HD_B))
    return jax.nn.silu(c).reshape(bsz, seq, D_B)


def setup_inputs(seed: int = 0) -> dict:
    key = jax.random.key(seed)
    ks = jax.random.split(key, 20)
    f32 = jnp.float32
    nrm = lambda k, shape, scale: jax.random.normal(k, shape, f32) * scale
    return {
        "x": jax.random.normal(ks[0], (BATCH, SEQ, D_MODEL), f32),
        "norm1_g": 1.0 + nrm(ks[1], (DEPTH, D_MODEL), 0.05),
        "w_in": nrm(ks[2], (DEPTH, D_MODEL, IN_COLS), D_MODEL ** -0.5),
        "sgu_ln_g": 1.0 + nrm(ks[3], (DEPTH, D_A), 0.05),
        "sgu_ln_b": nrm(ks[4], (DEPTH, D_A), 0.02),
        "sgu_w": nrm(ks[5], (DEPTH, H_A, CHUNK, CHUNK), CHUNK ** -0.5),
        "sgu_b": 1.0 + nrm(ks[6], (DEPTH, H_A, CHUNK), 0.1),
        "conv_w": nrm(ks[7], (DEPTH, CONV_W, D_B), CONV_W ** -0.5),
        "conv_b": nrm(ks[8], (DEPTH, D_B), 0.02),
        "conv_ln_g": 1.0 + nrm(ks[9], (DEPTH, D_B), 0.05),
        "conv_ln_b": nrm(ks[10], (DEPTH, D_B), 0.02),
        "w_out": nrm(ks[11], (DEPTH, D_MIX, D_MODEL), D_MIX ** -0.5),
        "norm2_g": 1.0 + nrm(ks[12], (DEPTH, D_MODEL), 0.05),
        "w_ff1": nrm(ks[13], (DEPTH, D_MODEL, D_FF), D_MODEL ** -0.5),
        "w_ff2": nrm(ks[14], (DEPTH, D_FF, D_MODEL), D_FF ** -0.5),
        "final_g": 1.0 + nrm(ks[15], (D_MODEL,), 0.05),
    }


def reference(x, norm1_g, w_in, sgu_ln_g, sgu_ln_b, sgu_w, sgu_b, conv_w, conv_b,
              conv_ln_g, conv_ln_b, w_out, norm2_g, w_ff1, w_ff2, final_g):
    for l in range(DEPTH):
        h = rms_norm(x, norm1_g[l])
        proj = jnp.einsum('bsd,dc->bsc', h, w_in[l])
        u_a = proj[..., :D_A]
        v_a = proj[..., D_A:2 * D_A]
        val_b = proj[..., 2 * D_A:2 * D_A + D_B]
        gate_b = proj[..., 2 * D_A + D_B:]
        a_out = spatial_gating(u_a, v_a, sgu_ln_g[l], sgu_ln_b[l], sgu_w[l], sgu_b[l])
        b_out = conformer_conv(val_b, gate_b, conv_w[l], conv_b[l],
                               conv_ln_g[l], conv_ln_b[l])
        mix = jnp.concatenate([a_out, b_out], axis=-1)
        x = x + jnp.einsum('bsc,cd->bsd', mix, w_out[l])
        h = rms_norm(x, norm2_g[l])
        f = jnp.square(jax.nn.relu(jnp.einsum('bsd,df->bsf', h, w_ff1[l])))
        x = x + jnp.einsum('bsf,fd->bsd', f, w_ff2[l])
    return rms_norm(x, final_g)
```

```python
import numpy as np
from contextlib import ExitStack
from functools import partial

import concourse.bass as bass
import concourse.mybir as mybir
from concourse.bass_utils import run_bass_kernel_spmd

F32 = mybir.dt.float32
BF16 = mybir.dt.bfloat16
AF = mybir.ActivationFunctionType
ALU = mybir.AluOpType

D = 1024
T = 4096
TB = 1024
NB = T // TB
DFF = 4096
EPS = 1e-6
NSLOT = 8
KW = 31
KDT = (8, 11)
KD = min(KDT)

O_G1, O_G2, O_GF, O_CB, O_CG, O_CBE, O_CW, O_WS, O_SLG, O_SLB = 0, 16, 32, 40, 48, 56, 64, 312, 1336, 2360
NPV = 3384


class Prog:
    def __init__(self, nc):
        self.nc = nc
        self.ops = []

    def op(self, eng, fn, reads=(), writes=(), dma=None):
        self.ops.append((eng, fn, tuple(reads), tuple(writes), dma))

    def emit(self, stack, final_wait_slots=()):
        nc = self.nc
        engs = {"pe": nc.tensor, "act": nc.scalar, "dve": nc.vector, "pool": nc.gpsimd, "sp": nc.sync}
        ops = self.ops
        n = len(ops)
        def stream(i):
            eng, _, _, _, dma = ops[i]
            return ("dma:" + dma) if dma is not None else eng

        last_w = {}
        readers = {}
        last_on_slot = {}
        deps = [None] * n
        for i, (eng, fn, reads, writes, dma) in enumerate(ops):
            d = {}

            def add(j):
                s = stream(j)
                if d.get(s, -1) < j:
                    d[s] = j

            for r in reads:
                j = last_w.get(r)
                if j is not None:
                    add(j)
            for w in writes:
                j = last_w.get(w)
                if j is not None:
                    add(j)
                for j in readers.get(w, {}).values():
                    add(j)
            if dma is not None and dma in last_on_slot:
                add(last_on_slot[dma])
            if eng == "pe" and dma is None:
                d.pop("pe", None)
            deps[i] = d
            me = stream(i)
            for r in reads:
                readers.setdefault(r, {})[me] = i
            for w in writes:
                last_w[w] = i
                readers[w] = {}
            if dma is not None:
                last_on_slot[dma] = i
        signaled = set()
        for d in deps:
            signaled.update(d.values())
        tick = {}
        event = {}
        for i in range(n):
            s = stream(i)
            if ops[i][4] is not None:
                tick[s] = tick.get(s, 0) + 16
                event[i] = (s, tick[s])
            elif i in signaled:
                tick[s] = tick.get(s, 0) + 1
                event[i] = (s, tick[s])
        sems = {}
        for s in sorted(tick):
            sems[s] = stack.enter_context(nc.semaphore("s_" + s.replace(":", "_")))
        waited = {e: {} for e in engs}
        nwait = 0
        for i, (eng, fn, reads, writes, dma) in enumerate(ops):
            E = engs[eng]
            for s, j in deps[i].items():
                _, v = event[j]
                if waited[eng].get(s, 0) >= v:
                    continue
                E.wait_ge(sems[s], v)
                waited[eng][s] = v
                nwait += 1
            ins = fn()
            if dma is not None:
                ins.then_inc(sems["dma:" + dma], 16)
            elif i in signaled:
                ins.then_inc(sems[eng], 1)
        for slot in final_wait_slots:
            s = "dma:" + slot
            if s in tick:
                nc.sync.wait_ge(sems[s], tick[s])
        self.stats = dict(n_ops=n, n_wait=nwait, ticks=dict(tick))


class Ring:
    def __init__(self, nslots):
        self.nslots = nslots
        self.free = [True] * nslots
        self.pending = []
        self.nreq = 0
        self.emitted = set()

    def request(self, fill_fn):
        fid = self.nreq
        self.nreq += 1
        self.pending.append((fid, fid % self.nslots, fill_fn))
        self.pump()
        return fid

    def pump(self):
        while self.pending and self.free[self.pending[0][1]]:
            fid, slot, fn = self.pending.pop(0)
            self.free[slot] = False
            fn(slot)
            self.emitted.add(fid)

    def slot(self, fid):
        assert fid in self.emitted, f"fill {fid} not yet emitted (ring schedule deadlock)"
        return fid % self.nslots

    def release(self, fid):
        self.free[fid % self.nslots] = True
        self.pump()


def build_program(n_layers=2, n_blocks=NB, do_ffn=True, do_mixer=True):
    nc = bass.Bass("TRN2", target_bir_lowering=False)
    xT = nc.dram_tensor("xT", [D, T], F32, kind="ExternalInput").ap()
    w_in = nc.dram_tensor("w_in", [2 * D, 2048], F32, kind="ExternalInput").ap()
    w_out = nc.dram_tensor("w_out", [2 * D, D], F32, kind="ExternalInput").ap()
    w_ff1 = nc.dram_tensor("w_ff1", [2 * D, DFF], F32, kind="ExternalInput").ap()
    w_ff2 = nc.dram_tensor("w_ff2", [2 * DFF, D], F32, kind="ExternalInput").ap()
    pv_d = nc.dram_tensor("pv", [128, NPV], F32, kind="ExternalInput").ap()
    sb_d = nc.dram_tensor("sb", [1, 1024], F32, kind="ExternalInput").ap()
    outT = nc.dram_tensor("outT", [D, T], F32, kind="ExternalOutput").ap()

    P = Prog(nc)
    st = ExitStack()
    sb = lambda name, shape, dt: st.enter_context(nc.sbuf_tensor(name, shape, dt))

    X = sb("X", [128, 8, TB], F32)
    H = sb("H", [128, 8, TB], BF16)
    MIX = sb("MIX", [128, 8, TB], BF16)
    U = sb("U", [128, 4, TB], BF16)
    VN = sb("VN", [128, 8, 512], BF16)
    Fb = sb("F", [128, 2, 8, 512], BF16)
    GLU = sb("GLU", [128, 4, TB + 32], BF16)
    TAIL = sb("TAIL", [128, 2, 4, 30], BF16)
    SIG = sb("SIG", [128, 2, 512], F32)
    RT = sb("RT", [128, 2, 512], F32)
    SQ = sb("SQ", [128, 4, 512], BF16)
    NSD = sb("NSD", [128, 2, 512], F32)
    W = sb("W", [128, NSLOT, 8, 512], BF16)
    PV = sb("PV", [128, NPV], F32)
    WM = sb("WM", [128, 8, 128], BF16)
    CM0 = sb("CM0", [128, 128], F32)
    CM = sb("CM", [128, 128], F32)
    ONESB = sb("ONESB", [128, 128], BF16)
    BIASL = sb("BIASL", [128, 128], BF16)
    BIASR = sb("BIASR", [128, 2, 1024], BF16)
    CBC = sb("CBC", [128, 8], F32)
    ST = sb("ST", [128, 8, 6], F32)
    MV = sb("MV", [128, 8, 2], F32)
    SDV = sb("SDV", [128, 8], F32)
    RV = sb("RV", [128, 8], F32)
    EPSC = sb("EPSC", [128, 1], F32)
    CMH = sb("CMH", [128, 128], BF16)
    PS = [st.enter_context(nc.psum_tensor(f"ps{i}", [128, 512], F32)) for i in range(8)]

    F_f32 = Fb[:].rearrange("p a b c -> p (a b c)").bitcast(F32).rearrange("p (t c) -> p t c", c=512)
    H_f32 = H[:].rearrange("p a b -> p (a b)").bitcast(F32).rearrange("p (t c) -> p t c", c=512)
    MIX_f32 = MIX[:].rearrange("p a b -> p (a b)").bitcast(F32).rearrange("p (t c) -> p t c", c=512)
    X2 = U[:].rearrange("p a b -> p (a b)").rearrange("p (t c) -> p t c", c=512)

    def GV(tc):
        return F_f32[:, tc, :], [("F", tc // 4, (tc % 4) * 2), ("F", tc // 4, (tc % 4) * 2 + 1)]

    def CCS(gi):
        return F_f32[:, gi, :], [("F", 0, 2 * gi), ("F", 0, 2 * gi + 1)]

    def SD(gi):
        return F_f32[:, 4 + gi, :], [("F", 1, 2 * gi), ("F", 1, 2 * gi + 1)]

    def x2v(dc):
        return X2[:, dc, :], [("U", dc // 2, dc % 2)]

    tsl = lambda tt: slice(tt * 512, (tt + 1) * 512)
    pvc = lambda off, i: PV[:, off + i: off + i + 1]

    _bank = [0]

    def next_bank():
        b = _bank[0]
        _bank[0] = (b + 1) % 6
        return b

    rtslots = [(SIG, 0, ("SIG", 0)), (SIG, 1, ("SIG", 1)), (RT, 0, ("RT", 0)), (RT, 1, ("RT", 1))]
    _rt = [0]

    def next_rt():
        r = rtslots[_rt[0] % 4]
        _rt[0] += 1
        return r[0][:, r[1], :], r[2]

    ring = Ring(NSLOT)

    def wkey(slot):
        return ("W", slot)

    for dc in range(8):
        P.op("sp", partial(nc.sync.dma_start, out=X[:, dc, 0:512], in_=xT[dc * 128:(dc + 1) * 128, 0:512]),
             writes=[("X", dc, 0)], dma=f"x{dc}_0")
    P.op("sp", lambda: nc.sync.dma_start(out=PV[:, 0:O_WS], in_=pv_d[:, 0:O_WS]), writes=["PVa"], dma="par")
    P.op("sp", lambda: nc.sync.dma_start(out=PV[:, O_WS:NPV], in_=pv_d[:, O_WS:NPV]), writes=["PVb"], dma="par2")
    P.op("pool", lambda: nc.gpsimd.memset(BIASR[:], 0.0), writes=["BIASR"])
    STG_row = RT[0:1, :, :].rearrange("p a b -> p (a b)")
    STG_row32 = RT[32:33, :, :].rearrange("p a b -> p (a b)")
    P.op("sp", lambda: nc.sync.dma_start(out=STG_row, in_=sb_d), writes=[("RT", 0), ("RT", 1)], dma="par")
    P.op("sp", lambda: nc.sync.dma_start(out=STG_row32, in_=sb_d), writes=[("RT", 0), ("RT", 1)], dma="par")
    P.op("pool", lambda: nc.gpsimd.memset(BIASL[:], 0.0), writes=["BIASL"])
    P.op("pool", lambda: nc.gpsimd.memset(BIASL[0:1, :], 1.0), writes=["BIASL"])
    P.op("pool", lambda: nc.gpsimd.memset(BIASL[32:33, :], 1.0), writes=["BIASL"])
    P.op("pool", lambda: nc.gpsimd.memset(ONESB[:], 1.0), writes=["ONESB"])
    P.op("pool", lambda: nc.gpsimd.memset(EPSC[:], EPS), writes=["EPSC"])
    P.op("pool", lambda: nc.gpsimd.memset(TAIL[:], 0.0), writes=[("TAIL", 0), ("TAIL", 1)])
    P.op("pool", lambda: nc.gpsimd.memset(CM0[:], -1.0 / 128.0), writes=["CM0"])
    P.op("pool", lambda: nc.gpsimd.affine_select(out=CM[:], in_=CM0[:], pattern=[[-1, 128]],
                                                 compare_op=ALU.not_equal, fill=1.0 - 1.0 / 128.0,
                                                 base=0, channel_multiplier=1),
         reads=["CM0"], writes=["CM"])
    P.op("pool", lambda: nc.gpsimd.tensor_scalar(out=CMH[:], in0=CM[:], scalar1=0.5, scalar2=0.0, op0=ALU.mult, op1=ALU.add),
         reads=["CM"], writes=["CMH"])
    P.op("pe", lambda: nc.tensor.matmul(PS[7][:, 0:8], lhsT=CM[:], rhs=PV[:, O_CB:O_CB + 8], start=True, stop=True),
         reads=["CM", "PVa"], writes=[("PS", 7)])
    P.op("dve", lambda: nc.vector.tensor_copy(out=CBC[:], in_=PS[7][:, 0:8]), reads=[("PS", 7)], writes=["CBC"])

    def fill_weight(dram, r0, c0, slot):
        src = dram[r0:r0 + 1024, c0:c0 + 512].rearrange("(kc p) c -> p kc c", p=128)
        P.op("pool", lambda: nc.gpsimd.dma_start(out=W[:, slot, :, :], in_=src),
             writes=[wkey(slot)], dma=f"w{slot}")

    def fill_taps(l, gi, slot):
        flat = W[:, slot, :, :].rearrange("p a b -> p (a b)")
        for k in range(KD, KW):
            col = O_CW + (l * 4 + gi) * KW + k
            P.op("pool", partial(nc.gpsimd.tensor_scalar, out=flat[:, k * 128:(k + 1) * 128], in0=CM[:],
                                 scalar1=PV[:, col:col + 1], scalar2=0.5, op0=ALU.mult, op1=ALU.mult),
                 reads=["CM", "PVa"], writes=[wkey(slot)])

    def mm_group(bank, pairs, out_ap=None, reads=()):
        out_ap = PS[bank][:] if out_ap is None else out_ap
        n = len(pairs)
        for i, (l_ap, r_ap, rk) in enumerate(pairs):
            P.op("pe", partial(nc.tensor.matmul, out_ap, lhsT=l_ap, rhs=r_ap, start=(i == 0), stop=(i == n - 1)),
                 reads=list(rk) + list(reads), writes=[("PS", bank)])

    def run(steps):
        for s in steps:
            s()

    def merge(a, b, wa=None):
        na, nb = len(a), len(b)
        wa = [1.0] * na if wa is None else list(wa)
        tot = float(sum(wa)) or 1.0
        ia = ib = 0
        ca = 0.0
        while ia < na or ib < nb:
            if ib >= nb or (ia < na and ca * nb <= ib * tot):
                a[ia]()
                ca += wa[ia]
                ia += 1
            else:
                b[ib]()
                ib += 1

    def merge_c(a, b):
        na, nb = len(a), len(b)
        tot = float(sum(x[1] for x in a)) or 1.0
        ia = ib = 0
        ca = 0.0
        while ia < na or ib < nb:
            take_a = None
            if ia >= na:
                take_a = False
            elif ib >= nb:
                take_a = True
            elif a[ia][2] > ib:
                assert b[ib][1] <= ia, "merge_c: unsatisfiable constraints"
                take_a = False
            elif b[ib][1] > ia:
                take_a = True
            else:
                take_a = ca * nb <= ib * tot
            if take_a:
                a[ia][0]()
                ca += a[ia][1]
                ia += 1
            else:
                b[ib][0]()
                ib += 1

    def norm_steps(tt, goff, l_idx, final_t0=None, src=None):
        ts = tsl(tt)
        steps = []
        if src is None:
            src = lambda dc: (X[:, dc, ts], [("X", dc, tt)])
        for dc in range(8):
            ap, keys = x2v(dc)
            s_ap, s_k = src(dc)
            steps.append(partial(P.op, "act", partial(nc.scalar.activation, out=ap, in_=s_ap, func=AF.Square),
                                 reads=s_k, writes=keys))
        bank = 6 + tt

        def ss():
            mm_group(bank, [(ONESB[:], x2v(dc)[0], ["ONESB"] + x2v(dc)[1]) for dc in range(8)])
            P.op("act", partial(nc.scalar.activation, out=NSD[:, tt, :], in_=PS[bank][:], func=AF.Ln,
                                bias=EPSC[:], scale=1.0 / D),
                 reads=[("PS", bank), "EPSC"], writes=[("NSD", tt)])
            P.op("act", partial(nc.scalar.activation, out=NSD[:, tt, :], in_=NSD[:, tt, :], func=AF.Exp, scale=-0.5),
                 reads=[("NSD", tt)], writes=[("NSD", tt)])
        steps.append(ss)
        for dc in range(8):
            if final_t0 is None:
                s_ap, s_k = src(dc)
                steps.append(partial(P.op, "dve", partial(
                    nc.vector.scalar_tensor_tensor, out=H[:, dc, ts], in0=s_ap,
                    scalar=pvc(goff, l_idx * 8 + dc), in1=NSD[:, tt, :], op0=ALU.mult, op1=ALU.mult),
                    reads=s_k + [("NSD", tt), "PVa"], writes=[("H", dc, tt)]))
            else:
                skeys = [("MIX", dc, 0), ("MIX", dc, 1)]
                steps.append(partial(P.op, "dve", partial(
                    nc.vector.scalar_tensor_tensor, out=MIX_f32[:, dc, :], in0=X[:, dc, ts],
                    scalar=pvc(goff, dc), in1=NSD[:, tt, :], op0=ALU.mult, op1=ALU.mult),
                    reads=[("X", dc, tt), ("NSD", tt), "PVa"], writes=skeys))
                if dc % 2 == 1:
                    hf = dc // 2
                    c0 = final_t0 + tt * 512
                    dst = outT.rearrange("(dc p) t -> p dc t", p=128)[:, hf * 2:hf * 2 + 2, c0:c0 + 512]
                    steps.append(partial(P.op, "sp", partial(nc.sync.dma_start, out=dst, in_=MIX_f32[:, hf * 2:hf * 2 + 2, :]),
                                         reads=[("MIX", d2, t2) for d2 in range(hf * 2, hf * 2 + 2) for t2 in range(2)],
                                         dma=f"o{hf}"))
        return steps

    def xload_steps(nb, tt):
        t0 = nb * TB + tt * 512
        return [partial(P.op, "sp", partial(nc.sync.dma_start, out=X[:, dc, tsl(tt)],
                                            in_=xT[dc * 128:(dc + 1) * 128, t0:t0 + 512]),
                        writes=[("X", dc, tt)], dma=f"x{dc}_{tt}") for dc in range(8)]

    VN_f32 = VN[:].rearrange("p a b -> p (a b)").bitcast(F32).rearrange("p (t c) -> p t c", c=512)
    GLU_f32 = GLU[:].rearrange("p a b -> p (a b)")[:, 0:4096].bitcast(F32).rearrange("p (t c) -> p t c", c=512)
    VN_keys = [("VN", tc) for tc in range(8)]
    GLU_keys = [("GLU", gi, part) for gi in range(4) for part in ("pad", 0, 1)]
    xT_v = xT.rearrange("(dc p) t -> p dc t", p=128)

    def slot_f32(slot):
        return W[:, slot, :, :].rearrange("p a b -> p (a b)").bitcast(F32).rearrange("p (t c) -> p t c", c=512)

    def stage0_load(nb1):
        c0 = nb1 * TB
        P.op("sp", partial(nc.sync.dma_start, out=VN_f32, in_=xT_v[:, 0:4, c0:c0 + 512]), writes=VN_keys, dma="xs0")
        P.op("sp", partial(nc.sync.dma_start, out=GLU_f32, in_=xT_v[:, 4:8, c0:c0 + 512]), writes=GLU_keys, dma="xs1")

    def stage1_fill(nb1, half, slot):
        c0 = nb1 * TB + 512
        P.op("sp", partial(nc.sync.dma_start, out=slot_f32(slot), in_=xT_v[:, 4 * half:4 * half + 4, c0:c0 + 512]),
             writes=[wkey(slot)], dma=f"w{slot}")

    def stage_src(tt, sfids):
        if tt == 0:
            return lambda dc: ((VN_f32[:, dc, :], VN_keys) if dc < 4 else (GLU_f32[:, dc - 4, :], GLU_keys))
        s_a, s_b = ring.slot(sfids[0]), ring.slot(sfids[1])
        return lambda dc: ((slot_f32(s_a)[:, dc, :], [wkey(s_a)]) if dc < 4 else (slot_f32(s_b)[:, dc - 4, :], [wkey(s_b)]))

    def stage_copy_steps(tt, sfids):
        ts = tsl(tt)
        if tt == 0:
            srcs = [(VN_f32, VN_keys), (GLU_f32, GLU_keys)]
        else:
            s_a, s_b = ring.slot(sfids[0]), ring.slot(sfids[1])
            srcs = [(slot_f32(s_a), [wkey(s_a)]), (slot_f32(s_b), [wkey(s_b)])]
        steps = []
        for hf, (ap, keys) in enumerate(srcs):
            steps.append(partial(P.op, "sp", partial(nc.sync.dma_start, out=X[:, 4 * hf:4 * hf + 4, ts], in_=ap),
                                 reads=keys, writes=[("X", dc, tt) for dc in range(4 * hf, 4 * hf + 4)],
                                 dma=f"xc{tt}{hf}"))
        return steps

    def win_v_steps(l, tt, fids):
        steps = []
        for tc in range(tt * 4, tt * 4 + 4):
            def g(tc=tc):
                s_v = ring.slot(fids[("wi", 1)])
                bank = next_bank()
                mm_group(bank, [(H[:, dc, tc * 128:(tc + 1) * 128], W[:, s_v, dc, :], [("H", dc, tt), wkey(s_v)])
                                for dc in range(8)])
                gv, gk = GV(tc)
                P.op("act", partial(nc.scalar.activation, out=gv, in_=PS[bank][:], func=AF.Gelu),
                     reads=[("PS", bank)], writes=gk)
                P.op("dve", partial(nc.vector.bn_stats, out=ST[:, tc, :], in_=gv), reads=gk, writes=[("ST", tc)])
                P.op("dve", partial(nc.vector.bn_aggr, out=MV[:, tc, :], in_=ST[:, tc, :]),
                     reads=[("ST", tc)], writes=[("MV", tc)])
            steps.append(g)
        return steps

    def win_vln(l, tt):
        tcs = slice(tt * 4, tt * 4 + 4)
        mvk = [("MV", tc) for tc in range(tt * 4, tt * 4 + 4)]
        P.op("act", partial(nc.scalar.activation, out=SDV[:, tcs], in_=MV[:, tcs, 1], func=AF.Ln, bias=EPSC[:], scale=1.0),
             reads=mvk + ["EPSC"], writes=[("SDV", tt)])
        P.op("act", partial(nc.scalar.activation, out=RV[:, tcs], in_=SDV[:, tcs], func=AF.Exp, scale=-0.5),
             reads=[("SDV", tt)], writes=[("RV", tt)])
        for tc in range(tt * 4, tt * 4 + 4):
            gv, gk = GV(tc)
            P.op("dve", partial(nc.vector.scalar_tensor_tensor, out=gv, in0=gv, scalar=MV[:, tc, 0:1],
                                in1=PV[:, O_SLG + l * 512:O_SLG + (l + 1) * 512], op0=ALU.subtract, op1=ALU.mult),
                 reads=gk + [("MV", tc), "PVb"], writes=gk)
            P.op("dve", partial(nc.vector.scalar_tensor_tensor, out=VN[:, tc, :], in0=gv, scalar=RV[:, tc:tc + 1],
                                in1=PV[:, O_SLB + l * 512:O_SLB + (l + 1) * 512], op0=ALU.mult, op1=ALU.add),
                 reads=gk + [("RV", tt), "PVb"], writes=[("VN", tc)])

    def win_glu_steps(l, tt, fids):
        ts = tsl(tt)
        steps = []
        for gi in range(4):
            def g(gi=gi):
                s_val = ring.slot(fids[("wi", 2)])
                s_gate = ring.slot(fids[("wi", 3)])
                b_val = next_bank()
                mm_group(b_val, [(W[:, s_val, dc, gi * 128:(gi + 1) * 128], H[:, dc, ts], [("H", dc, tt), wkey(s_val)])
                                 for dc in range(8)])
                b_gate = next_bank()
                mm_group(b_gate, [(W[:, s_gate, dc, gi * 128:(gi + 1) * 128], H[:, dc, ts], [("H", dc, tt), wkey(s_gate)])
                                  for dc in range(8)])
                P.op("act", partial(nc.scalar.activation, out=SIG[:, gi % 2, :], in_=PS[b_gate][:], func=AF.Tanh, scale=0.5),
                     reads=[("PS", b_gate)], writes=[("SIG", gi % 2)])
                P.op("dve", partial(nc.vector.scalar_tensor_tensor, out=GLU[:, gi, 30 + tt * 512:30 + (tt + 1) * 512],
                                    in0=SIG[:, gi % 2, :], scalar=1.0, in1=PS[b_val][:], op0=ALU.add, op1=ALU.mult),
                     reads=[("PS", b_val), ("SIG", gi % 2)], writes=[("GLU", gi, tt)])
            steps.append(g)
        return steps

    def win_u_steps(l, tt, fids):
        ts = tsl(tt)
        steps = []
        for cc in range(4):
            def g(cc=cc):
                s_u = ring.slot(fids[("wi", 0)])
                bank = next_bank()
                mm_group(bank, [(W[:, s_u, dc, cc * 128:(cc + 1) * 128], H[:, dc, ts], [("H", dc, tt), wkey(s_u)])
                                for dc in range(8)])
                P.op("act", partial(nc.scalar.activation, out=U[:, cc, ts], in_=PS[bank][:], func=AF.Gelu),
                     reads=[("PS", bank)], writes=[("U", cc, tt)])
            steps.append(g)
        return steps

    def sgu_steps(l, tt):
        return [partial(sgu_tc, l, tt, tc) for tc in range(tt * 4, tt * 4 + 4)]

    def sgu_tc(l, tt, tc):
        if True:
            bank = next_bank()
            for hd in range(4):
                o_ap = PS[bank][:, hd * 128:(hd + 1) * 128]
                P.op("pe", partial(nc.tensor.matmul, o_ap, lhsT=VN[:, tc, hd * 128:(hd + 1) * 128],
                                   rhs=WM[:, l * 4 + hd, :], start=True, stop=False),
                     reads=[("VN", tc), "WM"], writes=[("PS", bank)])
                P.op("pe", partial(nc.tensor.matmul, o_ap, lhsT=BIASL[:],
                                   rhs=BIASR[:, 0, (l * 4 + hd) * 128:(l * 4 + hd + 1) * 128],
                                   start=False, stop=True),
                     reads=["BIASL", "BIASR"], writes=[("PS", bank)])
            csl = slice(tc * 128, (tc + 1) * 128)
            P.op("dve", partial(nc.vector.tensor_tensor, out=MIX[:, 0:4, csl],
                                in0=PS[bank][:].rearrange("p (h t) -> p h t", h=4), in1=U[:, 0:4, csl], op=ALU.mult),
                 reads=[("PS", bank)] + [("U", cc, tt) for cc in range(4)],
                 writes=[("MIX", kc, tt) for kc in range(4)])

    def CCS2(tt, gi):
        return F_f32[:, tt * 4 + gi, :], [("F", tt, 2 * gi), ("F", tt, 2 * gi + 1)]

    sdslots = [(SIG, 0, ("SIG", 0)), (SIG, 1, ("SIG", 1)), (NSD, 0, ("NSD", 0)), (NSD, 1, ("NSD", 1))]

    def SD2(gi):
        r = sdslots[gi]
        return r[0][:, r[1], :], [r[2]]

    Fb_flat = Fb[:].rearrange("p a b c -> p (a b c)")

    def PD(tt, gi):
        o = (tt * 4 + gi) * 1024
        return Fb_flat[:, o:o + 512], [("F", tt, 2 * gi), ("F", tt, 2 * gi + 1)]

    def glu_keys(gi, tt):
        return [("GLU", gi, "pad"), ("GLU", gi, 0)] if tt == 0 else [("GLU", gi, 0), ("GLU", gi, 1)]

    def conv_dve_steps(l, tt, gis):
        steps = []
        KDv = KDT[tt]
        for k in range(KDv):
            for j, gi in enumerate(gis):
                acc = RT[:, j, :]
                akey = ("RT", j)
                src = GLU[:, gi, tt * 512 + k:tt * 512 + k + 512]
                w = pvc(O_CW, (l * 4 + gi) * KW + k)
                if k == 0:
                    steps.append(partial(P.op, "dve", partial(nc.vector.tensor_scalar, out=acc, in0=src, scalar1=w,
                                                              scalar2=None, op0=ALU.mult),
                                         reads=glu_keys(gi, tt) + ["PVa"], writes=[akey]))
                elif k < KDv - 1:
                    steps.append(partial(P.op, "dve", partial(nc.vector.scalar_tensor_tensor, out=acc, in0=src, scalar=w,
                                                              in1=acc, op0=ALU.mult, op1=ALU.add),
                                         reads=glu_keys(gi, tt) + ["PVa", akey], writes=[akey]))
                else:
                    pd_ap, pdk = PD(tt, gi)
                    steps.append(partial(P.op, "dve", partial(nc.vector.scalar_tensor_tensor, out=pd_ap, in0=src, scalar=w,
                                                              in1=acc, op0=ALU.mult, op1=ALU.add),
                                         reads=glu_keys(gi, tt) + ["PVa", akey], writes=pdk))
        return steps

    def conv_mm(l, tt, gi, fids):
        s_tap = ring.slot(fids[("tap", gi)])
        tapf = W[:, s_tap, :, :].rearrange("p a b -> p (a b)")
        bank = next_bank()
        gk = glu_keys(gi, tt)
        pd_ap, pdk = PD(tt, gi)
        mm_group(bank, [(tapf[:, k * 128:(k + 1) * 128], GLU[:, gi, tt * 512 + k:tt * 512 + k + 512],
                         gk + [wkey(s_tap)]) for k in range(KDT[tt], KW)] + [(CMH[:], pd_ap, ["CMH"] + pdk)])
        cc_ap, cck = CCS2(tt, gi)
        P.op("act", partial(nc.scalar.activation, out=cc_ap, in_=PS[bank][:], func=AF.Identity,
                            bias=CBC[:, l * 4 + gi:l * 4 + gi + 1], scale=1.0),
             reads=[("PS", bank), "CBC"], writes=cck)
        P.op("act", partial(nc.scalar.activation, out=SQ[:, gi, :], in_=cc_ap, func=AF.Square),
             reads=cck, writes=[("SQ", gi)])

    def conv_s2(tt, gi):
        vb = 6 + gi % 2
        mm_group(vb, [(ONESB[:], SQ[:, gi, :], ["ONESB", ("SQ", gi)])])
        sd_ap, sdk = SD2(gi)
        cc_ap, cck = CCS2(tt, gi)
        P.op("act", partial(nc.scalar.activation, out=sd_ap, in_=PS[vb][:], func=AF.Ln, bias=EPSC[:], scale=1.0 / 128.0),
             reads=[("PS", vb), "EPSC"], writes=sdk)
        P.op("act", partial(nc.scalar.activation, out=sd_ap, in_=sd_ap, func=AF.Exp, scale=-0.5), reads=sdk, writes=sdk)
        P.op("dve", partial(nc.vector.tensor_tensor, out=cc_ap, in0=cc_ap, in1=sd_ap, op=ALU.mult),
             reads=cck + sdk, writes=cck)

    def conv_silu(l, tt):
        for gi in range(4):
            cc_ap, cck = CCS2(tt, gi)
            P.op("act", partial(nc.scalar.activation, out=MIX[:, 4 + gi, tsl(tt)], in_=cc_ap, func=AF.Silu,
                                bias=pvc(O_CBE, l * 4 + gi), scale=pvc(O_CG, l * 4 + gi)),
                 reads=cck + ["PVa"], writes=[("MIX", 4 + gi, tt)])

    def wout_steps(l, tt, fids):
        ts = tsl(tt)
        steps = []
        for dco in range(8):
            def g(dco=dco):
                s_o = ring.slot(fids[("wo", dco // 4)])
                bank = next_bank()
                mm_group(bank, [(W[:, s_o, kc, (dco % 4) * 128:(dco % 4 + 1) * 128], MIX[:, kc, ts],
                                 [("MIX", kc, tt), wkey(s_o)]) for kc in range(8)])
                P.op("dve", partial(nc.vector.tensor_tensor, out=X[:, dco, ts], in0=PS[bank][:], in1=X[:, dco, ts], op=ALU.add),
                     reads=[("PS", bank), ("X", dco, tt)], writes=[("X", dco, tt)])
            steps.append(g)
        return steps

    def ff1_steps(q, tt, fids):
        ts = tsl(tt)
        steps = []
        for hc in range(8):
            def g(hc=hc):
                s1 = ring.slot(fids[("w1", q, hc // 4)])
                bank = next_bank()
                mm_group(bank, [(W[:, s1, dc, (hc % 4) * 128:(hc % 4 + 1) * 128], H[:, dc, ts],
                                 [("H", dc, tt), wkey(s1)]) for dc in range(8)])
                rt_ap, rtk = next_rt()
                P.op("act", partial(nc.scalar.activation, out=rt_ap, in_=PS[bank][:], func=AF.Relu),
                     reads=[("PS", bank)], writes=[rtk])
                P.op("dve", partial(nc.vector.tensor_tensor, out=Fb[:, tt, hc, :], in0=PS[bank][:], in1=rt_ap, op=ALU.mult),
                     reads=[("PS", bank), rtk], writes=[("F", tt, hc)])
            steps.append(g)
        return steps

    def ff2_steps(q, tt, fids):
        ts = tsl(tt)
        steps = []
        for dco in range(8):
            def g(dco=dco):
                s2 = ring.slot(fids[("w2", q, dco // 4)])
                bank = next_bank()
                mm_group(bank, [(W[:, s2, hc, (dco % 4) * 128:(dco % 4 + 1) * 128], Fb[:, tt, hc, :],
                                 [("F", tt, hc), wkey(s2)]) for hc in range(8)])
                P.op("dve", partial(nc.vector.tensor_tensor, out=X[:, dco, ts], in0=PS[bank][:], in1=X[:, dco, ts], op=ALU.add),
                     reads=[("PS", bank), ("X", dco, tt)], writes=[("X", dco, tt)])
            steps.append(g)
        return steps

    def request_layer(l):
        fids = {}
        if do_mixer:
            for cg in (1, 2, 3, 0):
                fids[("wi", cg)] = ring.request(partial(fill_weight, w_in, l * D, cg * 512))
            for gi in range(4):
                fids[("tap", gi)] = ring.request(partial(fill_taps, l, gi))
            for og in range(2):
                fids[("wo", og)] = ring.request(partial(fill_weight, w_out, l * D, og * 512))
        if do_ffn:
            for q in range(4):
                for hg in range(2):
                    fids[("w1", q, hg)] = ring.request(partial(fill_weight, w_ff1, l * D, q * 1024 + hg * 512))
                for og in range(2):
                    fids[("w2", q, og)] = ring.request(partial(fill_weight, w_ff2, l * DFF + q * 1024, og * 512))
        return fids

    def late_setup():
        P.op("dve", lambda: nc.vector.tensor_copy(out=BIASR[0:1, 0, :], in_=STG_row),
             reads=[("RT", 0), ("RT", 1)], writes=["BIASR"])
        P.op("dve", lambda: nc.vector.tensor_copy(out=BIASR[32:33, 1, :], in_=STG_row32),
             reads=[("RT", 0), ("RT", 1)], writes=["BIASR"])
        P.op("dve", lambda: nc.vector.tensor_tensor(out=BIASR[32:33, 0, :], in0=STG_row32, in1=BIASR[32:33, 1, :], op=ALU.subtract),
             reads=[("RT", 0), ("RT", 1), "BIASR"], writes=["BIASR"])
        P.op("pool", lambda: nc.gpsimd.affine_select(
            out=WM[:], in_=PV[:, O_WS:O_WS + 1024].rearrange("p (a t) -> p a t", t=128),
            pattern=[[0, 8], [1, 128]], compare_op=ALU.is_ge, fill=0.0, base=0, channel_multiplier=-1),
             reads=["PVb"], writes=["WM"])

    layer_steps = [(nb, l) for nb in range(n_blocks) for l in range(n_layers)]
    first_goff = O_G1 if do_mixer else O_G2

    def next_chain(i, tt):
        nb, l = layer_steps[i]
        if l < n_layers - 1:
            return norm_steps(tt, first_goff, l + 1)
        steps = norm_steps(tt, O_GF, 0, final_t0=nb * TB)
        if nb + 1 < n_blocks:
            steps += xload_steps(nb + 1, tt)
            steps += norm_steps(tt, first_goff, 0)
        return steps

    run(xload_steps(0, 1))
    run(norm_steps(0, first_goff, 0))
    carry = norm_steps(1, first_goff, 0)
    for i, (nb, l) in enumerate(layer_steps):
        fids = request_layer(l)
        if i == 0:
            late_setup()
        boundary = do_mixer and do_ffn and l == n_layers - 1 and nb + 1 < n_blocks
        sfids = None
        if boundary:
            sfids = [ring.request(partial(stage1_fill, nb + 1, hf)) for hf in range(2)]
        if do_mixer:
            P.op("dve", partial(nc.vector.tensor_copy, out=GLU[:, :, 0:30], in_=TAIL[:, l, :, :]),
                 reads=[("TAIL", l)], writes=[("GLU", gi, "pad") for gi in range(4)])
            a = win_v_steps(l, 0, fids) + [partial(win_vln, l, 0)] + win_glu_steps(l, 0, fids)
            merge(a, carry)
            carry = []
            c_mm0 = (KW - KDT[0] + 1) * 0.23
            c_mm1 = (KW - KDT[1] + 1) * 0.23
            A = []
            for st_ in win_u_steps(l, 0, fids):
                A.append([st_, 1.9, 0])
            for st_ in win_v_steps(l, 1, fids):
                A.append([st_, 1.0, 0])
            A.append([partial(win_vln, l, 1), 0.0, 0])
            for st_ in win_glu_steps(l, 1, fids):
                A.append([st_, 3.1, 0])
            n_after_glu1 = len(A)
            sg0 = sgu_steps(l, 0)
            sg1 = sgu_steps(l, 1)
            for j, st_ in enumerate(win_u_steps(l, 1, fids)):
                A.append([st_, 1.9, 0])
                A.append([sg0[j], 0.05, 0])

            def rel_wi():
                for cg in range(4):
                    ring.release(fids[("wi", cg)])
                P.op("dve", partial(nc.vector.tensor_copy, out=TAIL[:, l, :, :], in_=GLU[:, :, TB:TB + 30]),
                     reads=[("GLU", gi, 1) for gi in range(4)], writes=[("TAIL", l)])
            A.append([rel_wi, 0.0, 0])
            n0 = 2 * KDT[0]
            n1 = 2 * KDT[1]
            A.append([partial(conv_mm, l, 0, 0, fids), c_mm0, n0])
            A.append([sg1[0], 0.05, 0])
            A.append([partial(conv_mm, l, 0, 1, fids), c_mm0, n0])
            A.append([sg1[1], 0.05, 0])
            A.append([partial(conv_s2, 0, 0), 0.0, 0])
            A.append([partial(conv_mm, l, 0, 2, fids), c_mm0, 2 * n0])
            A.append([sg1[2], 0.05, 0])
            A.append([partial(conv_s2, 0, 1), 0.0, 0])
            A.append([partial(conv_mm, l, 0, 3, fids), c_mm0, 2 * n0])
            A.append([sg1[3], 0.05, 0])
            A.append([partial(conv_s2, 0, 2), 0.0, 0])
            A.append([partial(conv_mm, l, 1, 0, fids), c_mm1, 2 * n0 + n1])
            A.append([partial(conv_s2, 0, 3), 0.0, 0])
            A.append([partial(conv_silu, l, 0), 0.0, 0])
            A.append([partial(conv_mm, l, 1, 1, fids), c_mm1, 2 * n0 + n1])
            A.append([partial(conv_s2, 1, 0), 0.0, 0])
            B = [(st_, 0) for st_ in conv_dve_steps(l, 0, (0, 1)) + conv_dve_steps(l, 0, (2, 3))]
            B += [(st_, n_after_glu1) for st_ in conv_dve_steps(l, 1, (0, 1)) + conv_dve_steps(l, 1, (2, 3))]
            merge_c(A, B)
            conv_mm(l, 1, 2, fids); conv_s2(1, 1)
            conv_mm(l, 1, 3, fids); conv_s2(1, 2); conv_s2(1, 3)
            for gi in range(4):
                ring.release(fids[("tap", gi)])
            w0 = wout_steps(l, 0, fids)
            run(w0[:4])
            conv_silu(l, 1)
            run(w0[4:])
            if do_ffn:
                merge(wout_steps(l, 1, fids), norm_steps(0, O_G2, l))
                carry = norm_steps(1, O_G2, l)
            else:
                if i + 1 < len(layer_steps) or True:
                    merge(wout_steps(l, 1, fids), next_chain(i, 0))
                    carry = next_chain(i, 1)
            for og in range(2):
                ring.release(fids[("wo", og)])
        if boundary:
            stage0_load(nb + 1)
        if do_ffn:
            if not do_mixer and l > 0:
                pass
            for q in range(4):
                if q < 3:
                    merge(ff1_steps(q, 0, fids), carry)
                    carry = []
                    run(ff1_steps(q, 1, fids))
                    for hg in range(2):
                        ring.release(fids[("w1", q, hg)])
                    run(ff2_steps(q, 0, fids) + ff2_steps(q, 1, fids))
                    for og in range(2):
                        ring.release(fids[("w2", q, og)])
                elif not boundary:
                    run(ff1_steps(q, 0, fids) + ff2_steps(q, 0, fids))
                    tail_steps = ff1_steps(q, 1, fids) + ff2_steps(q, 1, fids)
                    merge(tail_steps[:11], next_chain(i, 0))
                    run(tail_steps[11:])
                    for hg in range(2):
                        ring.release(fids[("w1", q, hg)])
                    for og in range(2):
                        ring.release(fids[("w2", q, og)])
                    carry = next_chain(i, 1)
                else:
                    run(ff1_steps(q, 0, fids) + ff2_steps(q, 0, fids))
                    f1 = ff1_steps(q, 1, fids)

                    def rel_w1():
                        for hg in range(2):
                            ring.release(fids[("w1", 3, hg)])
                    A2 = [[st_, 1.9, 0] for st_ in f1] + [[rel_w1, 0.0, 0]] + [[st_, 1.9, 0] for st_ in ff2_steps(q, 1, fids)]
                    B2 = [(st_, 0) for st_ in norm_steps(0, first_goff, 0, src=stage_src(0, sfids))]
                    B2 += [(st_, 0) for st_ in norm_steps(0, O_GF, 0, final_t0=nb * TB) + stage_copy_steps(0, sfids)]
                    B2 += [(st_, len(f1)) for st_ in norm_steps(1, first_goff, 0, src=stage_src(1, sfids))]
                    merge_c(A2, B2)
                    for og in range(2):
                        ring.release(fids[("w2", q, og)])
                    carry = (norm_steps(1, O_GF, 0, final_t0=nb * TB) + stage_copy_steps(1, sfids)
                             + [partial(ring.release, sfids[0]), partial(ring.release, sfids[1])])
    run(carry)

    assert not ring.pending, "unfilled weight requests"
    P.emit(st, final_wait_slots=("o0", "o1", "o2", "o3"))
    P.stats["sbuf_free"] = nc.sbuf_bytes_remaining
    st.close()
    return nc, st, P


def prep_shared(inp):
    f = lambda a: np.ascontiguousarray(np.asarray(a, dtype=np.float32))
    pv = np.zeros((128, NPV), np.float32)
    pv[:, O_G1:O_G1 + 16] = f(inp["norm1_g"]).reshape(2, 8, 128).transpose(2, 0, 1).reshape(128, 16)
    pv[:, O_G2:O_G2 + 16] = f(inp["norm2_g"]).reshape(2, 8, 128).transpose(2, 0, 1).reshape(128, 16)
    pv[:, O_GF:O_GF + 8] = f(inp["final_g"]).reshape(8, 128).T
    pv[:, O_CB:O_CB + 8] = f(inp["conv_b"]).reshape(2, 4, 128).transpose(2, 0, 1).reshape(128, 8)
    pv[:, O_CG:O_CG + 8] = f(inp["conv_ln_g"]).reshape(2, 4, 128).transpose(2, 0, 1).reshape(128, 8)
    pv[:, O_CBE:O_CBE + 8] = f(inp["conv_ln_b"]).reshape(2, 4, 128).transpose(2, 0, 1).reshape(128, 8)
    pv[:, O_CW:O_CW + 248] = f(inp["conv_w"]).reshape(2, KW, 4, 128).transpose(3, 0, 2, 1).reshape(128, 248)
    pv[:, O_WS:O_WS + 1024] = f(inp["sgu_w"]).transpose(3, 0, 1, 2).reshape(128, 1024)
    pv[:, O_SLG:O_SLG + 1024] = np.broadcast_to(f(inp["sgu_ln_g"]).reshape(1, 1024), (128, 1024))
    pv[:, O_SLB:O_SLB + 1024] = np.broadcast_to(f(inp["sgu_ln_b"]).reshape(1, 1024), (128, 1024))
    return {
        "w_in": f(inp["w_in"]).reshape(2 * D, 2048),
        "w_out": f(inp["w_out"]).reshape(2 * D, D),
        "w_ff1": f(inp["w_ff1"]).reshape(2 * D, DFF),
        "w_ff2": f(inp["w_ff2"]).reshape(2 * DFF, D),
        "pv": pv,
        "sb": f(inp["sgu_b"]).reshape(1, 1024),
    }


_CACHE = {}


def kernel(**inputs):
    x = np.asarray(inputs["x"], dtype=np.float32)
    B = x.shape[0]
    shared = prep_shared(inputs)
    if "prog" not in _CACHE:
        _CACHE["prog"] = build_program()
    nc, st, P = _CACHE["prog"]
    in_maps = []
    for b in range(B):
        m = dict(shared)
        m["xT"] = np.ascontiguousarray(x[b].T)
        in_maps.append(m)
    res = run_bass_kernel_spmd(nc, in_maps, core_ids=list(range(B)))
    out = np.empty((B, T, D), np.float32)
    for b in range(B):
        out[b] = np.asarray(res.results[b]["outT"]).T
    return out
```

```python
import numpy as np
from contextlib import ExitStack
from functools import partial

import concourse.bass as bass
import concourse.mybir as mybir
from concourse.bass_utils import run_bass_kernel_spmd

F32 = mybir.dt.float32
BF16 = mybir.dt.bfloat16
AF = mybir.ActivationFunctionType
ALU = mybir.AluOpType

D = 1024
T = 4096
TB = 1024
NB = T // TB
DFF = 4096
EPS = 1e-6
NSLOT = 8
KW = 31
KDT = (8, 11)
KD = min(KDT)

O_G1, O_G2, O_GF, O_CB, O_CG, O_CBE, O_CW, O_WS, O_SLG, O_SLB = 0, 16, 32, 40, 48, 56, 64, 312, 1336, 2360
NPV = 3384


class Prog:
    def __init__(self, nc):
        self.nc = nc
        self.ops = []

    def op(self, eng, fn, reads=(), writes=(), dma=None):
        self.ops.append((eng, fn, tuple(reads), tuple(writes), dma))

    def emit(self, stack, final_wait_slots=()):
        nc = self.nc
        engs = {"pe": nc.tensor, "act": nc.scalar, "dve": nc.vector, "pool": nc.gpsimd, "sp": nc.sync}
        ops = self.ops
        n = len(ops)
        def stream(i):
            eng, _, _, _, dma = ops[i]
            return ("dma:" + dma) if dma is not None else eng

        last_w = {}
        readers = {}
        last_on_slot = {}
        deps = [None] * n
        for i, (eng, fn, reads, writes, dma) in enumerate(ops):
            d = {}

            def add(j):
                s = stream(j)
                if d.get(s, -1) < j:
                    d[s] = j

            for r in reads:
                j = last_w.get(r)
                if j is not None:
                    add(j)
            for w in writes:
                j = last_w.get(w)
                if j is not None:
                    add(j)
                for j in readers.get(w, {}).values():
                    add(j)
            if dma is not None and dma in last_on_slot:
                add(last_on_slot[dma])
            if eng == "pe" and dma is None:
                d.pop("pe", None)
            deps[i] = d
            me = stream(i)
            for r in reads:
                readers.setdefault(r, {})[me] = i
            for w in writes:
                last_w[w] = i
                readers[w] = {}
            if dma is not None:
                last_on_slot[dma] = i
        signaled = set()
        for d in deps:
            signaled.update(d.values())
        tick = {}
        event = {}
        for i in range(n):
            s = stream(i)
            if ops[i][4] is not None:
                tick[s] = tick.get(s, 0) + 16
                event[i] = (s, tick[s])
            elif i in signaled:
                tick[s] = tick.get(s, 0) + 1
                event[i] = (s, tick[s])
        sems = {}
        for s in sorted(tick):
            sems[s] = stack.enter_context(nc.semaphore("s_" + s.replace(":", "_")))
        waited = {e: {} for e in engs}
        nwait = 0
        for i, (eng, fn, reads, writes, dma) in enumerate(ops):
            E = engs[eng]
            for s, j in deps[i].items():
                _, v = event[j]
                if waited[eng].get(s, 0) >= v:
                    continue
                E.wait_ge(sems[s], v)
                waited[eng][s] = v
                nwait += 1
            ins = fn()
            if dma is not None:
                ins.then_inc(sems["dma:" + dma], 16)
            elif i in signaled:
                ins.then_inc(sems[eng], 1)
        for slot in final_wait_slots:
            s = "dma:" + slot
            if s in tick:
                nc.sync.wait_ge(sems[s], tick[s])
        self.stats = dict(n_ops=n, n_wait=nwait, ticks=dict(tick))


class Ring:
    def __init__(self, nslots):
        self.nslots = nslots
        self.free = [True] * nslots
        self.pending = []
        self.nreq = 0
        self.emitted = set()

    def request(self, fill_fn):
        fid = self.nreq
        self.nreq += 1
        self.pending.append((fid, fid % self.nslots, fill_fn))
        self.pump()
        return fid

    def pump(self):
        while self.pending and self.free[self.pending[0][1]]:
            fid, slot, fn = self.pending.pop(0)
            self.free[slot] = False
            fn(slot)
            self.emitted.add(fid)

    def slot(self, fid):
        assert fid in self.emitted, f"fill {fid} not yet emitted (ring schedule deadlock)"
        return fid % self.nslots

    def release(self, fid):
        self.free[fid % self.nslots] = True
        self.pump()


def build_program(n_layers=2, n_blocks=NB, do_ffn=True, do_mixer=True):
    nc = bass.Bass("TRN2", target_bir_lowering=False)
    xT = nc.dram_tensor("xT", [D, T], F32, kind="ExternalInput").ap()
    w_in = nc.dram_tensor("w_in", [2 * D, 2048], F32, kind="ExternalInput").ap()
    w_out = nc.dram_tensor("w_out", [2 * D, D], F32, kind="ExternalInput").ap()
    w_ff1 = nc.dram_tensor("w_ff1", [2 * D, DFF], F32, kind="ExternalInput").ap()
    w_ff2 = nc.dram_tensor("w_ff2", [2 * DFF, D], F32, kind="ExternalInput").ap()
    pv_d = nc.dram_tensor("pv", [128, NPV], F32, kind="ExternalInput").ap()
    sb_d = nc.dram_tensor("sb", [1, 1024], F32, kind="ExternalInput").ap()
    outT = nc.dram_tensor("outT", [D, T], F32, kind="ExternalOutput").ap()

    P = Prog(nc)
    st = ExitStack()
    sb = lambda name, shape, dt: st.enter_context(nc.sbuf_tensor(name, shape, dt))

    X = sb("X", [128, 8, TB], F32)
    H = sb("H", [128, 8, TB], BF16)
    MIX = sb("MIX", [128, 8, TB], BF16)
    U = sb("U", [128, 4, TB], BF16)
    VN = sb("VN", [128, 8, 512], BF16)
    Fb = sb("F", [128, 2, 8, 512], BF16)
    GLU = sb("GLU", [128, 4, TB + 32], BF16)
    TAIL = sb("TAIL", [128, 2, 4, 30], BF16)
    SIG = sb("SIG", [128, 2, 512], F32)
    RT = sb("RT", [128, 2, 512], F32)
    SQ = sb("SQ", [128, 4, 512], BF16)
    NSD = sb("NSD", [128, 2, 512], F32)
    W = sb("W", [128, NSLOT, 8, 512], BF16)
    PV = sb("PV", [128, NPV], F32)
    WM = sb("WM", [128, 8, 128], BF16)
    CM0 = sb("CM0", [128, 128], F32)
    CM = sb("CM", [128, 128], F32)
    ONESB = sb("ONESB", [128, 128], BF16)
    BIASL = sb("BIASL", [128, 128], BF16)
    BIASR = sb("BIASR", [128, 2, 1024], BF16)
    CBC = sb("CBC", [128, 8], F32)
    ST = sb("ST", [128, 8, 6], F32)
    MV = sb("MV", [128, 8, 2], F32)
    SDV = sb("SDV", [128, 8], F32)
    RV = sb("RV", [128, 8], F32)
    EPSC = sb("EPSC", [128, 1], F32)
    CMH = sb("CMH", [128, 128], BF16)
    PS = [st.enter_context(nc.psum_tensor(f"ps{i}", [128, 512], F32)) for i in range(8)]

    F_f32 = Fb[:].rearrange("p a b c -> p (a b c)").bitcast(F32).rearrange("p (t c) -> p t c", c=512)
    H_f32 = H[:].rearrange("p a b -> p (a b)").bitcast(F32).rearrange("p (t c) -> p t c", c=512)
    MIX_f32 = MIX[:].rearrange("p a b -> p (a b)").bitcast(F32).rearrange("p (t c) -> p t c", c=512)
    X2 = U[:].rearrange("p a b -> p (a b)").rearrange("p (t c) -> p t c", c=512)

    def GV(tc):
        return F_f32[:, tc, :], [("F", tc // 4, (tc % 4) * 2), ("F", tc // 4, (tc % 4) * 2 + 1)]

    def CCS(gi):
        return F_f32[:, gi, :], [("F", 0, 2 * gi), ("F", 0, 2 * gi + 1)]

    def SD(gi):
        return F_f32[:, 4 + gi, :], [("F", 1, 2 * gi), ("F", 1, 2 * gi + 1)]

    def x2v(dc):
        return X2[:, dc, :], [("U", dc // 2, dc % 2)]

    tsl = lambda tt: slice(tt * 512, (tt + 1) * 512)
    pvc = lambda off, i: PV[:, off + i: off + i + 1]

    _bank = [0]

    def next_bank():
        b = _bank[0]
        _bank[0] = (b + 1) % 6
        return b

    rtslots = [(SIG, 0, ("SIG", 0)), (SIG, 1, ("SIG", 1)), (RT, 0, ("RT", 0)), (RT, 1, ("RT", 1))]
    _rt = [0]

    def next_rt():
        r = rtslots[_rt[0] % 4]
        _rt[0] += 1
        return r[0][:, r[1], :], r[2]

    ring = Ring(NSLOT)

    def wkey(slot):
        return ("W", slot)

    for dc in range(8):
        P.op("sp", partial(nc.sync.dma_start, out=X[:, dc, 0:512], in_=xT[dc * 128:(dc + 1) * 128, 0:512]),
             writes=[("X", dc, 0)], dma=f"x{dc}_0")
    P.op("sp", lambda: nc.sync.dma_start(out=PV[:, 0:O_WS], in_=pv_d[:, 0:O_WS]), writes=["PVa"], dma="par")
    P.op("sp", lambda: nc.sync.dma_start(out=PV[:, O_WS:NPV], in_=pv_d[:, O_WS:NPV]), writes=["PVb"], dma="par2")
    P.op("pool", lambda: nc.gpsimd.memset(BIASR[:], 0.0), writes=["BIASR"])
    STG_row = RT[0:1, :, :].rearrange("p a b -> p (a b)")
    STG_row32 = RT[32:33, :, :].rearrange("p a b -> p (a b)")
    P.op("sp", lambda: nc.sync.dma_start(out=STG_row, in_=sb_d), writes=[("RT", 0), ("RT", 1)], dma="par")
    P.op("sp", lambda: nc.sync.dma_start(out=STG_row32, in_=sb_d), writes=[("RT", 0), ("RT", 1)], dma="par")
    P.op("pool", lambda: nc.gpsimd.memset(BIASL[:], 0.0), writes=["BIASL"])
    P.op("pool", lambda: nc.gpsimd.memset(BIASL[0:1, :], 1.0), writes=["BIASL"])
    P.op("pool", lambda: nc.gpsimd.memset(BIASL[32:33, :], 1.0), writes=["BIASL"])
    P.op("pool", lambda: nc.gpsimd.memset(ONESB[:], 1.0), writes=["ONESB"])
    P.op("pool", lambda: nc.gpsimd.memset(EPSC[:], EPS), writes=["EPSC"])
    P.op("pool", lambda: nc.gpsimd.memset(TAIL[:], 0.0), writes=[("TAIL", 0), ("TAIL", 1)])
    P.op("pool", lambda: nc.gpsimd.memset(CM0[:], -1.0 / 128.0), writes=["CM0"])
    P.op("pool", lambda: nc.gpsimd.affine_select(out=CM[:], in_=CM0[:], pattern=[[-1, 128]],
                                                 compare_op=ALU.not_equal, fill=1.0 - 1.0 / 128.0,
                                                 base=0, channel_multiplier=1),
         reads=["CM0"], writes=["CM"])
    P.op("pool", lambda: nc.gpsimd.tensor_scalar(out=CMH[:], in0=CM[:], scalar1=0.5, scalar2=0.0, op0=ALU.mult, op1=ALU.add),
         reads=["CM"], writes=["CMH"])
    P.op("pe", lambda: nc.tensor.matmul(PS[7][:, 0:8], lhsT=CM[:], rhs=PV[:, O_CB:O_CB + 8], start=True, stop=True),
         reads=["CM", "PVa"], writes=[("PS", 7)])
    P.op("dve", lambda: nc.vector.tensor_copy(out=CBC[:], in_=PS[7][:, 0:8]), reads=[("PS", 7)], writes=["CBC"])

    def fill_weight(dram, r0, c0, slot):
        src = dram[r0:r0 + 1024, c0:c0 + 512].rearrange("(kc p) c -> p kc c", p=128)
        P.op("pool", lambda: nc.gpsimd.dma_start(out=W[:, slot, :, :], in_=src),
             writes=[wkey(slot)], dma=f"w{slot}")

    def fill_taps(l, gi, slot):
        flat = W[:, slot, :, :].rearrange("p a b -> p (a b)")
        for k in range(KD, KW):
            col = O_CW + (l * 4 + gi) * KW + k
            P.op("pool", partial(nc.gpsimd.tensor_scalar, out=flat[:, k * 128:(k + 1) * 128], in0=CM[:],
                                 scalar1=PV[:, col:col + 1], scalar2=0.5, op0=ALU.mult, op1=ALU.mult),
                 reads=["CM", "PVa"], writes=[wkey(slot)])

    def mm_group(bank, pairs, out_ap=None, reads=()):
        out_ap = PS[bank][:] if out_ap is None else out_ap
        n = len(pairs)
        for i, (l_ap, r_ap, rk) in enumerate(pairs):
            P.op("pe", partial(nc.tensor.matmul, out_ap, lhsT=l_ap, rhs=r_ap, start=(i == 0), stop=(i == n - 1)),
                 reads=list(rk) + list(reads), writes=[("PS", bank)])

    def run(steps):
        for s in steps:
            s()

    def merge(a, b, wa=None):
        na, nb = len(a), len(b)
        wa = [1.0] * na if wa is None else list(wa)
        tot = float(sum(wa)) or 1.0
        ia = ib = 0
        ca = 0.0
        while ia < na or ib < nb:
            if ib >= nb or (ia < na and ca * nb <= ib * tot):
                a[ia]()
                ca += wa[ia]
                ia += 1
            else:
                b[ib]()
                ib += 1

    def merge_c(a, b):
        na, nb = len(a), len(b)
        tot = float(sum(x[1] for x in a)) or 1.0
        ia = ib = 0
        ca = 0.0
        while ia < na or ib < nb:
            take_a = None
            if ia >= na:
                take_a = False
            elif ib >= nb:
                take_a = True
            elif a[ia][2] > ib:
                assert b[ib][1] <= ia, "merge_c: unsatisfiable constraints"
                take_a = False
            elif b[ib][1] > ia:
                take_a = True
            else:
                take_a = ca * nb <= ib * tot
            if take_a:
                a[ia][0]()
                ca += a[ia][1]
                ia += 1
            else:
                b[ib][0]()
                ib += 1

    def norm_steps(tt, goff, l_idx, final_t0=None, src=None):
        ts = tsl(tt)
        steps = []
        if src is None:
            src = lambda dc: (X[:, dc, ts], [("X", dc, tt)])
        for dc in range(8):
            ap, keys = x2v(dc)
            s_ap, s_k = src(dc)
            steps.append(partial(P.op, "act", partial(nc.scalar.activation, out=ap, in_=s_ap, func=AF.Square),
                                 reads=s_k, writes=keys))
        bank = 6 + tt

        def ss():
            mm_group(bank, [(ONESB[:], x2v(dc)[0], ["ONESB"] + x2v(dc)[1]) for dc in range(8)])
            P.op("act", partial(nc.scalar.activation, out=NSD[:, tt, :], in_=PS[bank][:], func=AF.Ln,
                                bias=EPSC[:], scale=1.0 / D),
                 reads=[("PS", bank), "EPSC"], writes=[("NSD", tt)])
            P.op("act", partial(nc.scalar.activation, out=NSD[:, tt, :], in_=NSD[:, tt, :], func=AF.Exp, scale=-0.5),
                 reads=[("NSD", tt)], writes=[("NSD", tt)])
        steps.append(ss)
        for dc in range(8):
            if final_t0 is None:
                s_ap, s_k = src(dc)
                steps.append(partial(P.op, "dve", partial(
                    nc.vector.scalar_tensor_tensor, out=H[:, dc, ts], in0=s_ap,
                    scalar=pvc(goff, l_idx * 8 + dc), in1=NSD[:, tt, :], op0=ALU.mult, op1=ALU.mult),
                    reads=s_k + [("NSD", tt), "PVa"], writes=[("H", dc, tt)]))
            else:
                skeys = [("MIX", dc, 0), ("MIX", dc, 1)]
                steps.append(partial(P.op, "dve", partial(
                    nc.vector.scalar_tensor_tensor, out=MIX_f32[:, dc, :], in0=X[:, dc, ts],
                    scalar=pvc(goff, dc), in1=NSD[:, tt, :], op0=ALU.mult, op1=ALU.mult),
                    reads=[("X", dc, tt), ("NSD", tt), "PVa"], writes=skeys))
                if dc % 4 == 3:
                    hf = dc // 4
                    c0 = final_t0 + tt * 512
                    dst = outT.rearrange("(dc p) t -> p dc t", p=128)[:, hf * 4:hf * 4 + 4, c0:c0 + 512]
                    steps.append(partial(P.op, "sp", partial(nc.sync.dma_start, out=dst, in_=MIX_f32[:, hf * 4:hf * 4 + 4, :]),
                                         reads=[("MIX", d2, t2) for d2 in range(hf * 4, hf * 4 + 4) for t2 in range(2)],
                                         dma=f"o{hf}"))
        return steps

    def xload_steps(nb, tt):
        t0 = nb * TB + tt * 512
        return [partial(P.op, "sp", partial(nc.sync.dma_start, out=X[:, dc, tsl(tt)],
                                            in_=xT[dc * 128:(dc + 1) * 128, t0:t0 + 512]),
                        writes=[("X", dc, tt)], dma=f"x{dc}_{tt}") for dc in range(8)]

    VN_f32 = VN[:].rearrange("p a b -> p (a b)").bitcast(F32).rearrange("p (t c) -> p t c", c=512)
    GLU_f32 = GLU[:].rearrange("p a b -> p (a b)")[:, 0:4096].bitcast(F32).rearrange("p (t c) -> p t c", c=512)
    VN_keys = [("VN", tc) for tc in range(8)]
    GLU_keys = [("GLU", gi, part) for gi in range(4) for part in ("pad", 0, 1)]
    xT_v = xT.rearrange("(dc p) t -> p dc t", p=128)

    def slot_f32(slot):
        return W[:, slot, :, :].rearrange("p a b -> p (a b)").bitcast(F32).rearrange("p (t c) -> p t c", c=512)

    def stage0_load(nb1):
        c0 = nb1 * TB
        P.op("sp", partial(nc.sync.dma_start, out=VN_f32, in_=xT_v[:, 0:4, c0:c0 + 512]), writes=VN_keys, dma="xs0")
        P.op("sp", partial(nc.sync.dma_start, out=GLU_f32, in_=xT_v[:, 4:8, c0:c0 + 512]), writes=GLU_keys, dma="xs1")

    def stage1_fill(nb1, half, slot):
        c0 = nb1 * TB + 512
        P.op("sp", partial(nc.sync.dma_start, out=slot_f32(slot), in_=xT_v[:, 4 * half:4 * half + 4, c0:c0 + 512]),
             writes=[wkey(slot)], dma=f"w{slot}")

    def stage_src(tt, sfids):
        if tt == 0:
            return lambda dc: ((VN_f32[:, dc, :], VN_keys) if dc < 4 else (GLU_f32[:, dc - 4, :], GLU_keys))
        s_a, s_b = ring.slot(sfids[0]), ring.slot(sfids[1])
        return lambda dc: ((slot_f32(s_a)[:, dc, :], [wkey(s_a)]) if dc < 4 else (slot_f32(s_b)[:, dc - 4, :], [wkey(s_b)]))

    def stage_copy_steps(tt, sfids):
        ts = tsl(tt)
        if tt == 0:
            srcs = [(VN_f32, VN_keys), (GLU_f32, GLU_keys)]
        else:
            s_a, s_b = ring.slot(sfids[0]), ring.slot(sfids[1])
            srcs = [(slot_f32(s_a), [wkey(s_a)]), (slot_f32(s_b), [wkey(s_b)])]
        steps = []
        for hf, (ap, keys) in enumerate(srcs):
            steps.append(partial(P.op, "sp", partial(nc.sync.dma_start, out=X[:, 4 * hf:4 * hf + 4, ts], in_=ap),
                                 reads=keys, writes=[("X", dc, tt) for dc in range(4 * hf, 4 * hf + 4)],
                                 dma=f"xc{tt}{hf}"))
        return steps

    def win_v_steps(l, tt, fids):
        steps = []
        for tc in range(tt * 4, tt * 4 + 4):
            def g(tc=tc):
                s_v = ring.slot(fids[("wi", 1)])
                bank = next_bank()
                mm_group(bank, [(H[:, dc, tc * 128:(tc + 1) * 128], W[:, s_v, dc, :], [("H", dc, tt), wkey(s_v)])
                                for dc in range(8)])
                gv, gk = GV(tc)
                P.op("act", partial(nc.scalar.activation, out=gv, in_=PS[bank][:], func=AF.Gelu),
                     reads=[("PS", bank)], writes=gk)
                P.op("dve", partial(nc.vector.bn_stats, out=ST[:, tc, :], in_=gv), reads=gk, writes=[("ST", tc)])
                P.op("dve", partial(nc.vector.bn_aggr, out=MV[:, tc, :], in_=ST[:, tc, :]),
                     reads=[("ST", tc)], writes=[("MV", tc)])
            steps.append(g)
        return steps

    def win_vln(l, tt):
        tcs = slice(tt * 4, tt * 4 + 4)
        mvk = [("MV", tc) for tc in range(tt * 4, tt * 4 + 4)]
        P.op("act", partial(nc.scalar.activation, out=SDV[:, tcs], in_=MV[:, tcs, 1], func=AF.Ln, bias=EPSC[:], scale=1.0),
             reads=mvk + ["EPSC"], writes=[("SDV", tt)])
        P.op("act", partial(nc.scalar.activation, out=RV[:, tcs], in_=SDV[:, tcs], func=AF.Exp, scale=-0.5),
             reads=[("SDV", tt)], writes=[("RV", tt)])
        for tc in range(tt * 4, tt * 4 + 4):
            gv, gk = GV(tc)
            P.op("dve", partial(nc.vector.scalar_tensor_tensor, out=gv, in0=gv, scalar=MV[:, tc, 0:1],
                                in1=PV[:, O_SLG + l * 512:O_SLG + (l + 1) * 512], op0=ALU.subtract, op1=ALU.mult),
                 reads=gk + [("MV", tc), "PVb"], writes=gk)
            P.op("dve", partial(nc.vector.scalar_tensor_tensor, out=VN[:, tc, :], in0=gv, scalar=RV[:, tc:tc + 1],
                                in1=PV[:, O_SLB + l * 512:O_SLB + (l + 1) * 512], op0=ALU.mult, op1=ALU.add),
                 reads=gk + [("RV", tt), "PVb"], writes=[("VN", tc)])

    def win_glu_steps(l, tt, fids):
        ts = tsl(tt)
        steps = []
        for gi in range(4):
            def g(gi=gi):
                s_val = ring.slot(fids[("wi", 2)])
                s_gate = ring.slot(fids[("wi", 3)])
                b_val = next_bank()
                mm_group(b_val, [(W[:, s_val, dc, gi * 128:(gi + 1) * 128], H[:, dc, ts], [("H", dc, tt), wkey(s_val)])
                                 for dc in range(8)])
                b_gate = next_bank()
                mm_group(b_gate, [(W[:, s_gate, dc, gi * 128:(gi + 1) * 128], H[:, dc, ts], [("H", dc, tt), wkey(s_gate)])
                                  for dc in range(8)])
                P.op("act", partial(nc.scalar.activation, out=SIG[:, gi % 2, :], in_=PS[b_gate][:], func=AF.Tanh, scale=0.5),
                     reads=[("PS", b_gate)], writes=[("SIG", gi % 2)])
                P.op("dve", partial(nc.vector.scalar_tensor_tensor, out=GLU[:, gi, 30 + tt * 512:30 + (tt + 1) * 512],
                                    in0=SIG[:, gi % 2, :], scalar=1.0, in1=PS[b_val][:], op0=ALU.add, op1=ALU.mult),
                     reads=[("PS", b_val), ("SIG", gi % 2)], writes=[("GLU", gi, tt)])
            steps.append(g)
        return steps

    def win_u_steps(l, tt, fids):
        ts = tsl(tt)
        steps = []
        for cc in range(4):
            def g(cc=cc):
                s_u = ring.slot(fids[("wi", 0)])
                bank = next_bank()
                mm_group(bank, [(W[:, s_u, dc, cc * 128:(cc + 1) * 128], H[:, dc, ts], [("H", dc, tt), wkey(s_u)])
                                for dc in range(8)])
                P.op("act", partial(nc.scalar.activation, out=U[:, cc, ts], in_=PS[bank][:], func=AF.Gelu),
                     reads=[("PS", bank)], writes=[("U", cc, tt)])
            steps.append(g)
        return steps

    def sgu_steps(l, tt):
        return [partial(sgu_tc, l, tt, tc) for tc in range(tt * 4, tt * 4 + 4)]

    def sgu_tc(l, tt, tc):
        if True:
            bank = next_bank()
            for hd in range(4):
                o_ap = PS[bank][:, hd * 128:(hd + 1) * 128]
                P.op("pe", partial(nc.tensor.matmul, o_ap, lhsT=VN[:, tc, hd * 128:(hd + 1) * 128],
                                   rhs=WM[:, l * 4 + hd, :], start=True, stop=False),
                     reads=[("VN", tc), "WM"], writes=[("PS", bank)])
                P.op("pe", partial(nc.tensor.matmul, o_ap, lhsT=BIASL[:],
                                   rhs=BIASR[:, 0, (l * 4 + hd) * 128:(l * 4 + hd + 1) * 128],
                                   start=False, stop=True),
                     reads=["BIASL", "BIASR"], writes=[("PS", bank)])
            csl = slice(tc * 128, (tc + 1) * 128)
            P.op("dve", partial(nc.vector.tensor_tensor, out=MIX[:, 0:4, csl],
                                in0=PS[bank][:].rearrange("p (h t) -> p h t", h=4), in1=U[:, 0:4, csl], op=ALU.mult),
                 reads=[("PS", bank)] + [("U", cc, tt) for cc in range(4)],
                 writes=[("MIX", kc, tt) for kc in range(4)])

    def CCS2(tt, gi):
        return F_f32[:, tt * 4 + gi, :], [("F", tt, 2 * gi), ("F", tt, 2 * gi + 1)]

    sdslots = [(SIG, 0, ("SIG", 0)), (SIG, 1, ("SIG", 1)), (NSD, 0, ("NSD", 0)), (NSD, 1, ("NSD", 1))]

    def SD2(gi):
        r = sdslots[gi]
        return r[0][:, r[1], :], [r[2]]

    Fb_flat = Fb[:].rearrange("p a b c -> p (a b c)")

    def PD(tt, gi):
        o = (tt * 4 + gi) * 1024
        return Fb_flat[:, o:o + 512], [("F", tt, 2 * gi), ("F", tt, 2 * gi + 1)]

    def glu_keys(gi, tt):
        return [("GLU", gi, "pad"), ("GLU", gi, 0)] if tt == 0 else [("GLU", gi, 0), ("GLU", gi, 1)]

    def conv_dve_steps(l, tt, gis):
        steps = []
        KDv = KDT[tt]
        for k in range(KDv):
            for j, gi in enumerate(gis):
                acc = RT[:, j, :]
                akey = ("RT", j)
                src = GLU[:, gi, tt * 512 + k:tt * 512 + k + 512]
                w = pvc(O_CW, (l * 4 + gi) * KW + k)
                if k == 0:
                    steps.append(partial(P.op, "dve", partial(nc.vector.tensor_scalar, out=acc, in0=src, scalar1=w,
                                                              scalar2=None, op0=ALU.mult),
                                         reads=glu_keys(gi, tt) + ["PVa"], writes=[akey]))
                elif k < KDv - 1:
                    steps.append(partial(P.op, "dve", partial(nc.vector.scalar_tensor_tensor, out=acc, in0=src, scalar=w,
                                                              in1=acc, op0=ALU.mult, op1=ALU.add),
                                         reads=glu_keys(gi, tt) + ["PVa", akey], writes=[akey]))
                else:
                    pd_ap, pdk = PD(tt, gi)
                    steps.append(partial(P.op, "dve", partial(nc.vector.scalar_tensor_tensor, out=pd_ap, in0=src, scalar=w,
                                                              in1=acc, op0=ALU.mult, op1=ALU.add),
                                         reads=glu_keys(gi, tt) + ["PVa", akey], writes=pdk))
        return steps

    def conv_mm(l, tt, gi, fids):
        s_tap = ring.slot(fids[("tap", gi)])
        tapf = W[:, s_tap, :, :].rearrange("p a b -> p (a b)")
        bank = next_bank()
        gk = glu_keys(gi, tt)
        pd_ap, pdk = PD(tt, gi)
        mm_group(bank, [(tapf[:, k * 128:(k + 1) * 128], GLU[:, gi, tt * 512 + k:tt * 512 + k + 512],
                         gk + [wkey(s_tap)]) for k in range(KDT[tt], KW)] + [(CMH[:], pd_ap, ["CMH"] + pdk)])
        cc_ap, cck = CCS2(tt, gi)
        P.op("act", partial(nc.scalar.activation, out=cc_ap, in_=PS[bank][:], func=AF.Identity,
                            bias=CBC[:, l * 4 + gi:l * 4 + gi + 1], scale=1.0),
             reads=[("PS", bank), "CBC"], writes=cck)
        P.op("act", partial(nc.scalar.activation, out=SQ[:, gi, :], in_=cc_ap, func=AF.Square),
             reads=cck, writes=[("SQ", gi)])

    def conv_s2(tt, gi):
        vb = 6 + gi % 2
        mm_group(vb, [(ONESB[:], SQ[:, gi, :], ["ONESB", ("SQ", gi)])])
        sd_ap, sdk = SD2(gi)
        cc_ap, cck = CCS2(tt, gi)
        P.op("act", partial(nc.scalar.activation, out=sd_ap, in_=PS[vb][:], func=AF.Ln, bias=EPSC[:], scale=1.0 / 128.0),
             reads=[("PS", vb), "EPSC"], writes=sdk)
        P.op("act", partial(nc.scalar.activation, out=sd_ap, in_=sd_ap, func=AF.Exp, scale=-0.5), reads=sdk, writes=sdk)
        P.op("dve", partial(nc.vector.tensor_tensor, out=cc_ap, in0=cc_ap, in1=sd_ap, op=ALU.mult),
             reads=cck + sdk, writes=cck)

    def conv_silu(l, tt):
        for gi in range(4):
            cc_ap, cck = CCS2(tt, gi)
            P.op("act", partial(nc.scalar.activation, out=MIX[:, 4 + gi, tsl(tt)], in_=cc_ap, func=AF.Silu,
                                bias=pvc(O_CBE, l * 4 + gi), scale=pvc(O_CG, l * 4 + gi)),
                 reads=cck + ["PVa"], writes=[("MIX", 4 + gi, tt)])

    def wout_steps(l, tt, fids):
        ts = tsl(tt)
        steps = []
        for dco in range(8):
            def g(dco=dco):
                s_o = ring.slot(fids[("wo", dco // 4)])
                bank = next_bank()
                mm_group(bank, [(W[:, s_o, kc, (dco % 4) * 128:(dco % 4 + 1) * 128], MIX[:, kc, ts],
                                 [("MIX", kc, tt), wkey(s_o)]) for kc in range(8)])
                P.op("dve", partial(nc.vector.tensor_tensor, out=X[:, dco, ts], in0=PS[bank][:], in1=X[:, dco, ts], op=ALU.add),
                     reads=[("PS", bank), ("X", dco, tt)], writes=[("X", dco, tt)])
            steps.append(g)
        return steps

    def ff1_steps(q, tt, fids):
        ts = tsl(tt)
        steps = []
        for hc in range(8):
            def g(hc=hc):
                s1 = ring.slot(fids[("w1", q, hc // 4)])
                bank = next_bank()
                mm_group(bank, [(W[:, s1, dc, (hc % 4) * 128:(hc % 4 + 1) * 128], H[:, dc, ts],
                                 [("H", dc, tt), wkey(s1)]) for dc in range(8)])
                rt_ap, rtk = next_rt()
                P.op("act", partial(nc.scalar.activation, out=rt_ap, in_=PS[bank][:], func=AF.Relu),
                     reads=[("PS", bank)], writes=[rtk])
                P.op("dve", partial(nc.vector.tensor_tensor, out=Fb[:, tt, hc, :], in0=PS[bank][:], in1=rt_ap, op=ALU.mult),
                     reads=[("PS", bank), rtk], writes=[("F", tt, hc)])
            steps.append(g)
        return steps

    def ff2_steps(q, tt, fids):
        ts = tsl(tt)
        steps = []
        for dco in range(8):
            def g(dco=dco):
                s2 = ring.slot(fids[("w2", q, dco // 4)])
                bank = next_bank()
                mm_group(bank, [(W[:, s2, hc, (dco % 4) * 128:(dco % 4 + 1) * 128], Fb[:, tt, hc, :],
                                 [("F", tt, hc), wkey(s2)]) for hc in range(8)])
                P.op("dve", partial(nc.vector.tensor_tensor, out=X[:, dco, ts], in0=PS[bank][:], in1=X[:, dco, ts], op=ALU.add),
                     reads=[("PS", bank), ("X", dco, tt)], writes=[("X", dco, tt)])
            steps.append(g)
        return steps

    def request_layer(l):
        fids = {}
        if do_mixer:
            for cg in (1, 2, 3, 0):
                fids[("wi", cg)] = ring.request(partial(fill_weight, w_in, l * D, cg * 512))
            for gi in range(4):
                fids[("tap", gi)] = ring.request(partial(fill_taps, l, gi))
            for og in range(2):
                fids[("wo", og)] = ring.request(partial(fill_weight, w_out, l * D, og * 512))
        if do_ffn:
            for q in range(4):
                for hg in range(2):
                    fids[("w1", q, hg)] = ring.request(partial(fill_weight, w_ff1, l * D, q * 1024 + hg * 512))
                for og in range(2):
                    fids[("w2", q, og)] = ring.request(partial(fill_weight, w_ff2, l * DFF + q * 1024, og * 512))
        return fids

    def late_setup():
        P.op("dve", lambda: nc.vector.tensor_copy(out=BIASR[0:1, 0, :], in_=STG_row),
             reads=[("RT", 0), ("RT", 1)], writes=["BIASR"])
        P.op("dve", lambda: nc.vector.tensor_copy(out=BIASR[32:33, 1, :], in_=STG_row32),
             reads=[("RT", 0), ("RT", 1)], writes=["BIASR"])
        P.op("dve", lambda: nc.vector.tensor_tensor(out=BIASR[32:33, 0, :], in0=STG_row32, in1=BIASR[32:33, 1, :], op=ALU.subtract),
             reads=[("RT", 0), ("RT", 1), "BIASR"], writes=["BIASR"])
        P.op("pool", lambda: nc.gpsimd.affine_select(
            out=WM[:], in_=PV[:, O_WS:O_WS + 1024].rearrange("p (a t) -> p a t", t=128),
            pattern=[[0, 8], [1, 128]], compare_op=ALU.is_ge, fill=0.0, base=0, channel_multiplier=-1),
             reads=["PVb"], writes=["WM"])

    layer_steps = [(nb, l) for nb in range(n_blocks) for l in range(n_layers)]
    first_goff = O_G1 if do_mixer else O_G2

    def next_chain(i, tt):
        nb, l = layer_steps[i]
        if l < n_layers - 1:
            return norm_steps(tt, first_goff, l + 1)
        steps = norm_steps(tt, O_GF, 0, final_t0=nb * TB)
        if nb + 1 < n_blocks:
            steps += xload_steps(nb + 1, tt)
            steps += norm_steps(tt, first_goff, 0)
        return steps

    run(xload_steps(0, 1))
    run(norm_steps(0, first_goff, 0))
    carry = norm_steps(1, first_goff, 0)
    for i, (nb, l) in enumerate(layer_steps):
        fids = request_layer(l)
        if i == 0:
            late_setup()
        boundary = do_mixer and do_ffn and l == n_layers - 1 and nb + 1 < n_blocks
        sfids = None
        if boundary:
            sfids = [ring.request(partial(stage1_fill, nb + 1, hf)) for hf in range(2)]
        if do_mixer:
            P.op("dve", partial(nc.vector.tensor_copy, out=GLU[:, :, 0:30], in_=TAIL[:, l, :, :]),
                 reads=[("TAIL", l)], writes=[("GLU", gi, "pad") for gi in range(4)])
            a = win_v_steps(l, 0, fids) + [partial(win_vln, l, 0)] + win_glu_steps(l, 0, fids)
            merge(a, carry)
            carry = []
            c_mm0 = (KW - KDT[0] + 1) * 0.23
            c_mm1 = (KW - KDT[1] + 1) * 0.23
            A = []
            for st_ in win_u_steps(l, 0, fids):
                A.append([st_, 1.9, 0])
            for st_ in win_v_steps(l, 1, fids):
                A.append([st_, 1.0, 0])
            A.append([partial(win_vln, l, 1), 0.0, 0])
            for st_ in win_glu_steps(l, 1, fids):
                A.append([st_, 3.1, 0])
            n_after_glu1 = len(A)
            sg0 = sgu_steps(l, 0)
            sg1 = sgu_steps(l, 1)
            for j, st_ in enumerate(win_u_steps(l, 1, fids)):
                A.append([st_, 1.9, 0])
                A.append([sg0[j], 0.05, 0])

            def rel_wi():
                for cg in range(4):
                    ring.release(fids[("wi", cg)])
                P.op("dve", partial(nc.vector.tensor_copy, out=TAIL[:, l, :, :], in_=GLU[:, :, TB:TB + 30]),
                     reads=[("GLU", gi, 1) for gi in range(4)], writes=[("TAIL", l)])
            A.append([rel_wi, 0.0, 0])
            n0 = 2 * KDT[0]
            n1 = 2 * KDT[1]
            A.append([partial(conv_mm, l, 0, 0, fids), c_mm0, n0])
            A.append([sg1[0], 0.05, 0])
            A.append([partial(conv_mm, l, 0, 1, fids), c_mm0, n0])
            A.append([sg1[1], 0.05, 0])
            A.append([partial(conv_s2, 0, 0), 0.0, 0])
            A.append([partial(conv_mm, l, 0, 2, fids), c_mm0, 2 * n0])
            A.append([sg1[2], 0.05, 0])
            A.append([partial(conv_s2, 0, 1), 0.0, 0])
            A.append([partial(conv_mm, l, 0, 3, fids), c_mm0, 2 * n0])
            A.append([sg1[3], 0.05, 0])
            A.append([partial(conv_s2, 0, 2), 0.0, 0])
            A.append([partial(conv_mm, l, 1, 0, fids), c_mm1, 2 * n0 + n1])
            A.append([partial(conv_s2, 0, 3), 0.0, 0])
            A.append([partial(conv_silu, l, 0), 0.0, 0])
            A.append([partial(conv_mm, l, 1, 1, fids), c_mm1, 2 * n0 + n1])
            A.append([partial(conv_s2, 1, 0), 0.0, 0])
            B = [(st_, 0) for st_ in conv_dve_steps(l, 0, (0, 1)) + conv_dve_steps(l, 0, (2, 3))]
            B += [(st_, n_after_glu1) for st_ in conv_dve_steps(l, 1, (0, 1)) + conv_dve_steps(l, 1, (2, 3))]
            merge_c(A, B)
            conv_mm(l, 1, 2, fids); conv_s2(1, 1)
            conv_mm(l, 1, 3, fids); conv_s2(1, 2); conv_s2(1, 3)
            for gi in range(4):
                ring.release(fids[("tap", gi)])
            w0 = wout_steps(l, 0, fids)
            run(w0[:4])
            conv_silu(l, 1)
            run(w0[4:])
            if do_ffn:
                w1s = wout_steps(l, 1, fids)
                merge(w1s[:6], norm_steps(0, O_G2, l))
                run(w1s[6:])
                carry = norm_steps(1, O_G2, l)
            else:
                if i + 1 < len(layer_steps) or True:
                    merge(wout_steps(l, 1, fids), next_chain(i, 0))
                    carry = next_chain(i, 1)
            for og in range(2):
                ring.release(fids[("wo", og)])
        if boundary:
            stage0_load(nb + 1)
        if do_ffn:
            if not do_mixer and l > 0:
                pass
            for q in range(4):
                if q < 3:
                    merge(ff1_steps(q, 0, fids), carry)
                    carry = []
                    run(ff1_steps(q, 1, fids))
                    for hg in range(2):
                        ring.release(fids[("w1", q, hg)])
                    run(ff2_steps(q, 0, fids) + ff2_steps(q, 1, fids))
                    for og in range(2):
                        ring.release(fids[("w2", q, og)])
                elif not boundary:
                    run(ff1_steps(q, 0, fids) + ff2_steps(q, 0, fids))
                    tail_steps = ff1_steps(q, 1, fids) + ff2_steps(q, 1, fids)
                    merge(tail_steps[:11], next_chain(i, 0))
                    run(tail_steps[11:])
                    for hg in range(2):
                        ring.release(fids[("w1", q, hg)])
                    for og in range(2):
                        ring.release(fids[("w2", q, og)])
                    carry = next_chain(i, 1)
                else:
                    run(ff1_steps(q, 0, fids) + ff2_steps(q, 0, fids))
                    f1 = ff1_steps(q, 1, fids)

                    def rel_w1():
                        for hg in range(2):
                            ring.release(fids[("w1", 3, hg)])
                    A2 = [[st_, 1.9, 0] for st_ in f1] + [[rel_w1, 0.0, 0]] + [[st_, 1.9, 0] for st_ in ff2_steps(q, 1, fids)]
                    B2 = [(st_, 0) for st_ in norm_steps(0, first_goff, 0, src=stage_src(0, sfids))]
                    B2 += [(st_, 0) for st_ in norm_steps(0, O_GF, 0, final_t0=nb * TB) + stage_copy_steps(0, sfids)]
                    B2 += [(st_, len(f1)) for st_ in norm_steps(1, first_goff, 0, src=stage_src(1, sfids))]
                    merge_c(A2, B2)
                    for og in range(2):
                        ring.release(fids[("w2", q, og)])
                    carry = (norm_steps(1, O_GF, 0, final_t0=nb * TB) + stage_copy_steps(1, sfids)
                             + [partial(ring.release, sfids[0]), partial(ring.release, sfids[1])])
    run(carry)

    assert not ring.pending, "unfilled weight requests"
    P.emit(st, final_wait_slots=("o0", "o1"))
    P.stats["sbuf_free"] = nc.sbuf_bytes_remaining
    st.close()
    return nc, st, P


def prep_shared(inp):
    f = lambda a: np.ascontiguousarray(np.asarray(a, dtype=np.float32))
    pv = np.zeros((128, NPV), np.float32)
    pv[:, O_G1:O_G1 + 16] = f(inp["norm1_g"]).reshape(2, 8, 128).transpose(2, 0, 1).reshape(128, 16)
    pv[:, O_G2:O_G2 + 16] = f(inp["norm2_g"]).reshape(2, 8, 128).transpose(2, 0, 1).reshape(128, 16)
    pv[:, O_GF:O_GF + 8] = f(inp["final_g"]).reshape(8, 128).T
    pv[:, O_CB:O_CB + 8] = f(inp["conv_b"]).reshape(2, 4, 128).transpose(2, 0, 1).reshape(128, 8)
    pv[:, O_CG:O_CG + 8] = f(inp["conv_ln_g"]).reshape(2, 4, 128).transpose(2, 0, 1).reshape(128, 8)
    pv[:, O_CBE:O_CBE + 8] = f(inp["conv_ln_b"]).reshape(2, 4, 128).transpose(2, 0, 1).reshape(128, 8)
    pv[:, O_CW:O_CW + 248] = f(inp["conv_w"]).reshape(2, KW, 4, 128).transpose(3, 0, 2, 1).reshape(128, 248)
    pv[:, O_WS:O_WS + 1024] = f(inp["sgu_w"]).transpose(3, 0, 1, 2).reshape(128, 1024)
    pv[:, O_SLG:O_SLG + 1024] = np.broadcast_to(f(inp["sgu_ln_g"]).reshape(1, 1024), (128, 1024))
    pv[:, O_SLB:O_SLB + 1024] = np.broadcast_to(f(inp["sgu_ln_b"]).reshape(1, 1024), (128, 1024))
    return {
        "w_in": f(inp["w_in"]).reshape(2 * D, 2048),
        "w_out": f(inp["w_out"]).reshape(2 * D, D),
        "w_ff1": f(inp["w_ff1"]).reshape(2 * D, DFF),
        "w_ff2": f(inp["w_ff2"]).reshape(2 * DFF, D),
        "pv": pv,
        "sb": f(inp["sgu_b"]).reshape(1, 1024),
    }


_CACHE = {}


def kernel(**inputs):
    x = np.asarray(inputs["x"], dtype=np.float32)
    B = x.shape[0]
    shared = prep_shared(inputs)
    if "prog" not in _CACHE:
        _CACHE["prog"] = build_program()
    nc, st, P = _CACHE["prog"]
    in_maps = []
    for b in range(B):
        m = dict(shared)
        m["xT"] = np.ascontiguousarray(x[b].T)
        in_maps.append(m)
    res = run_bass_kernel_spmd(nc, in_maps, core_ids=list(range(B)))
    out = np.empty((B, T, D), np.float32)
    for b in range(B):
        out[b] = np.asarray(res.results[b]["outT"]).T
    return out
```
